# Optimizing a Trainium2 kernel written in Bass

```python
import math
import jax, jax.numpy as jnp
from jax import lax
import numpy as np

D_MODEL = 1024
BATCH = 4
SEQ = 4096
DEPTH = 2

CHUNK = 64
Q_BLOCK = 128
N_A_LAYERS = DEPTH // 2
N_B_LAYERS = DEPTH - N_A_LAYERS
HEAD_DIM = 64
FOX_HEADS = D_MODEL // HEAD_DIM
DIFF_HEADS = D_MODEL // (2 * HEAD_DIM)
DIFF_V_DIM = 2 * HEAD_DIM
D_FF = 2816
CONV_WIDTH = 3
ROPE_THETA = 10000.0
LN_EPS = 1e-5
RMS_EPS = 1e-5
DN_ALPHA = (2 * DEPTH) ** 0.25
DN_BETA = (8 * DEPTH) ** -0.25

kernel_name = 'yoco_fox_diffattn_convffn_deepnorm'


def _layer_norm(x, g, b):
    xf = x.astype(jnp.float32)
    mu = jnp.mean(xf, axis=-1, keepdims=True)
    var = jnp.mean(jnp.square(xf - mu), axis=-1, keepdims=True)
    y = (xf - mu) * lax.rsqrt(var + LN_EPS)
    return (y * g.astype(jnp.float32) + b.astype(jnp.float32)).astype(x.dtype)


def _rope_tables(seq_len):
    inv_freq = 1.0 / (ROPE_THETA ** (jnp.arange(0, HEAD_DIM, 2, dtype=jnp.float32) / HEAD_DIM))
    ang = jnp.arange(seq_len, dtype=jnp.float32)[:, None] * inv_freq[None, :]
    return jnp.cos(ang), jnp.sin(ang)


def _apply_rope(t, cos, sin):
    half = t.shape[-1] // 2
    t1, t2 = t[..., :half], t[..., half:]
    c = cos.astype(t.dtype)
    s = sin.astype(t.dtype)
    return jnp.concatenate([t1 * c - t2 * s, t2 * c + t1 * s], axis=-1)


def _heads(t, n_heads, d):
    b, s, _ = t.shape
    return t.reshape(b, s, n_heads, d).transpose(0, 2, 1, 3)


def _merge_blocks(o):
    nb, b, h, qb, dv = o.shape
    return o.transpose(1, 0, 3, 2, 4).reshape(b, nb * qb, h * dv)


def _fox_attention(x, w_in, b_f, w_out):
    b, s, d = x.shape
    proj = x @ w_in
    q = _heads(proj[..., :d], FOX_HEADS, HEAD_DIM)
    k = _heads(proj[..., d:2 * d], FOX_HEADS, HEAD_DIM)
    v = _heads(proj[..., 2 * d:3 * d], FOX_HEADS, HEAD_DIM)
    f_logit = proj[..., 3 * d:].astype(jnp.float32) + b_f.astype(jnp.float32)
    c = jnp.cumsum(jax.nn.log_sigmoid(f_logit), axis=1).transpose(0, 2, 1)
    scale = HEAD_DIM ** -0.5
    kpos = jnp.arange(s)

    def block(i):
        start = i * Q_BLOCK
        qb = lax.dynamic_slice_in_dim(q, start, Q_BLOCK, axis=2)
        cb = lax.dynamic_slice_in_dim(c, start, Q_BLOCK, axis=2)
        logits = jnp.einsum('bhqd,bhkd->bhqk', qb, k).astype(jnp.float32) * scale
        logits = logits + cb[..., :, None] - c[..., None, :]
        qpos = start + jnp.arange(Q_BLOCK)
        mask = kpos[None, :] <= qpos[:, None]
        p = jax.nn.softmax(jnp.where(mask, logits, -jnp.inf), axis=-1).astype(v.dtype)
        return jnp.einsum('bhqk,bhkd->bhqd', p, v)

    o = lax.map(block, jnp.arange(s // Q_BLOCK))
    return _merge_blocks(o) @ w_out


def _shared_kv(x, w_kv, cos, sin):
    b, s, _ = x.shape
    kw = 2 * DIFF_HEADS * HEAD_DIM
    proj = x @ w_kv
    k = proj[..., :kw].reshape(b, s, 2, DIFF_HEADS, HEAD_DIM).transpose(2, 0, 3, 1, 4)
    k1 = _apply_rope(k[0], cos, sin)
    k2 = _apply_rope(k[1], cos, sin)
    v = _heads(proj[..., kw:], DIFF_HEADS, DIFF_V_DIM)
    return k1, k2, v


def _diff_attention(x, k1, k2, v, w_q, lam_p, subln_g, w_out, lambda_init, cos, sin):
    b, s, _ = x.shape
    qp = (x @ w_q).reshape(b, s, 2, DIFF_HEADS, HEAD_DIM).transpose(2, 0, 3, 1, 4)
    q1 = _apply_rope(qp[0], cos, sin)
    q2 = _apply_rope(qp[1], cos, sin)
    lp = lam_p.astype(jnp.float32)
    lam = jnp.exp(jnp.sum(lp[0] * lp[1])) - jnp.exp(jnp.sum(lp[2] * lp[3])) + lambda_init
    scale = HEAD_DIM ** -0.5
    kpos = jnp.arange(s)

    def block(i):
        start = i * Q_BLOCK
        qb1 = lax.dynamic_slice_in_dim(q1, start, Q_BLOCK, axis=2)
        qb2 = lax.dynamic_slice_in_dim(q2, start, Q_BLOCK, axis=2)
        qpos = start + jnp.arange(Q_BLOCK)
        mask = kpos[None, :] < ((qpos // CHUNK) + 1)[:, None] * CHUNK
        s1 = jnp.einsum('bhqd,bhkd->bhqk', qb1, k1).astype(jnp.float32) * scale
        s2 = jnp.einsum('bhqd,bhkd->bhqk', qb2, k2).astype(jnp.float32) * scale
        p1 = jax.nn.softmax(jnp.where(mask, s1, -jnp.inf), axis=-1)
        p2 = jax.nn.softmax(jnp.where(mask, s2, -jnp.inf), axis=-1)
        a = (p1 - lam * p2).astype(v.dtype)
        return jnp.einsum('bhqk,bhkd->bhqd', a, v)

    o = lax.map(block, jnp.arange(s // Q_BLOCK)).astype(jnp.float32)
    o = o * lax.rsqrt(jnp.mean(jnp.square(o), axis=-1, keepdims=True) + RMS_EPS)
    o = (o * subln_g.astype(jnp.float32) * (1.0 - lambda_init)).astype(x.dtype)
    return _merge_blocks(o) @ w_out


def _conv_ffn(x, w_in, conv_w, conv_b, w_out):
    s = x.shape[1]
    u = x @ w_in
    up = jnp.pad(u, ((0, 0), (CONV_WIDTH - 1, 0), (0, 0)))
    u = sum(conv_w[j] * up[:, j:j + s] for j in range(CONV_WIDTH)) + conv_b
    g, val = u[..., :D_FF], u[..., D_FF:]
    return (jax.nn.silu(g) * val) @ w_out


def setup_inputs(seed: int = 0) -> dict:
    key = jax.random.key(seed)
    ks = jax.random.split(key, 20)
    d = D_MODEL
    f32 = jnp.float32
    nrm = lambda k, shape, sc: jax.random.normal(k, shape, f32) * sc
    return {
        'x': jax.random.normal(ks[0], (BATCH, SEQ, d), f32),
        'a_w_in': nrm(ks[1], (N_A_LAYERS, d, 3 * d + FOX_HEADS), d ** -0.5),
        'a_b_f': jax.random.uniform(ks[2], (N_A_LAYERS, FOX_HEADS), f32, 1.0, 6.0),
        'a_w_out': nrm(ks[3], (N_A_LAYERS, d, d), DN_BETA * d ** -0.5),
        'kv_w': nrm(ks[4], (d, 2 * DIFF_HEADS * HEAD_DIM + DIFF_HEADS * DIFF_V_DIM), d ** -0.5),
        'b_w_q': nrm(ks[5], (N_B_LAYERS, d, 2 * DIFF_HEADS * HEAD_DIM), d ** -0.5),
        'b_lambda': nrm(ks[6], (N_B_LAYERS, 4, HEAD_DIM), 0.1),
        'b_subln_g': 1.0 + nrm(ks[7], (N_B_LAYERS, DIFF_V_DIM), 0.02),
        'b_w_out': nrm(ks[8], (N_B_LAYERS, DIFF_HEADS * DIFF_V_DIM, d), DN_BETA * d ** -0.5),
        'ffn_w_in': nrm(ks[9], (DEPTH, d, 2 * D_FF), d ** -0.5),
        'ffn_conv_w': nrm(ks[10], (DEPTH, CONV_WIDTH, 2 * D_FF), CONV_WIDTH ** -0.5),
        'ffn_conv_b': nrm(ks[11], (DEPTH, 2 * D_FF), 0.02),
        'ffn_w_out': nrm(ks[12], (DEPTH, D_FF, d), DN_BETA * D_FF ** -0.5),
        'ln_attn_g': 1.0 + nrm(ks[13], (DEPTH, d), 0.02),
        'ln_attn_b': nrm(ks[14], (DEPTH, d), 0.02),
        'ln_ffn_g': 1.0 + nrm(ks[15], (DEPTH, d), 0.02),
        'ln_ffn_b': nrm(ks[16], (DEPTH, d), 0.02),
    }


def reference(x, a_w_in, a_b_f, a_w_out, kv_w, b_w_q, b_lambda, b_subln_g, b_w_out,
              ffn_w_in, ffn_conv_w, ffn_conv_b, ffn_w_out,
              ln_attn_g, ln_attn_b, ln_ffn_g, ln_ffn_b):
    cos, sin = _rope_tables(x.shape[1])
    k1 = k2 = v = None
    for layer in range(DEPTH):
        if layer < N_A_LAYERS:
            h = _fox_attention(x, a_w_in[layer], a_b_f[layer], a_w_out[layer])
        else:
            j = layer - N_A_LAYERS
            lambda_init = 0.8 - 0.6 * math.exp(-0.3 * layer)
            h = _diff_attention(x, k1, k2, v, b_w_q[j], b_lambda[j], b_subln_g[j], b_w_out[j],
                                lambda_init, cos, sin)
        x = _layer_norm(DN_ALPHA * x + h, ln_attn_g[layer], ln_attn_b[layer])
        f = _conv_ffn(x, ffn_w_in[layer], ffn_conv_w[layer], ffn_conv_b[layer], ffn_w_out[layer])
        x = _layer_norm(DN_ALPHA * x + f, ln_ffn_g[layer], ln_ffn_b[layer])
        if layer == N_A_LAYERS - 1:
            k1, k2, v = _shared_kv(x, kv_w, cos, sin)
    return x
```

```python
import math
from contextlib import ExitStack

import numpy as np
import concourse.bass as bass
import concourse.mybir as mybir
from concourse.bass_utils import run_bass_kernel_spmd

F32 = mybir.dt.float32
BF16 = mybir.dt.bfloat16
ALU = mybir.AluOpType
AF = mybir.ActivationFunctionType
AX = mybir.AxisListType

D = 1024
NKC = 8
SEQ = 4096
NS = 34
PA0, OA0, PB0, OB0 = 0, 8, 16, 25
TK = NS * 128
NOWN = 17
TQ = NOWN * 128
D_FF = 2816
NFC = 22
DN_ALPHA = 4.0 ** 0.25
LN_EPS = 1e-5
RMS_EPS = 1e-5
LAMBDA_INIT = 0.8 - 0.6 * math.exp(-0.3 * 1)
NEG = -30000.0
SEM_CAP = 4000
RG = [[0, 1], [2, 3], [4, 5], [6, 7]]

OWN_A = {0: list(range(0, 8)), 1: list(range(7, 15))}
OWN_B = {0: list(range(23, 32)), 1: list(range(15, 24))}
FFN_GROUPS = [(0, 6), (6, 12), (12, 17)]
QCHUNKS = [(0, 1024), (1024, 640), (1664, 512)]


class Buf:
    __slots__ = ("name", "w", "r")

    def __init__(self, name):
        self.name = name
        self.w = None
        self.r = []


class DmaSem:
    def __init__(self, trk, name):
        self.dsem = trk.new_sem("d_" + name)
        self.dcount = 0
        trk.dsems.append(self)


class Tracker:
    ENGS = ("pe", "act", "dve", "pool", "sp")

    def __init__(self, nc, es):
        self.nc = nc
        self.es = es
        self.lists = {e: [] for e in self.ENGS}
        self.count = {e: 0 for e in self.ENGS}
        self.sems = {e: [] for e in self.ENGS}
        self.waited = {e: {} for e in self.ENGS}
        self.nsem = 0
        self.semobjs = {}
        self.nwait = 0
        self.dsems = []

    def new_sem(self, name):
        self.nsem += 1
        s = self.es.enter_context(self.nc.semaphore("%s_%d" % (name, self.nsem)))
        key = ("S", self.nsem)
        self.semobjs[key] = s
        return key

    def _eng_sem(self, eng, k):
        while len(self.sems[eng]) <= k:
            self.sems[eng].append(self.new_sem("p_%s_%d" % (eng, len(self.sems[eng]))))
        return self.sems[eng][k]

    def eng_dep(self, eng, idx):
        k = (idx - 1) // SEM_CAP
        return (self._eng_sem(eng, k), (idx - 1) % SEM_CAP + 1, eng, idx)

    def _wait(self, eng, dep):
        key, val = dep[0], dep[1]
        if self.waited[eng].get(key, 0) >= val:
            return
        self.waited[eng][key] = val
        self.lists[eng].append(("wait", key, val))
        self.nwait += 1

    def _deps(self, eng, reads, writes):
        for b in reads:
            if b.w is not None:
                d = b.w
                if not (len(d) == 4 and d[2] == eng and eng == "pe"):
                    self._wait(eng, d)
        for b in writes:
            for d in ([b.w] if b.w is not None else []) + b.r:
                if len(d) == 4 and d[2] == eng:
                    continue
                self._wait(eng, d)

    def op(self, eng, fn, reads=(), writes=()):
        self._deps(eng, reads, writes)
        self.count[eng] += 1
        me = self.eng_dep(eng, self.count[eng])
        self.lists[eng].append(("op", fn, me[0]))
        for b in reads:
            b.r.append(me)
        for b in writes:
            b.w = me
            b.r = []
        return me

    def dma(self, queue, fn, sem_owner, reads=(), writes=(), inc=16):
        self._deps(queue, reads, writes)
        sem_owner.dcount += (1 if inc is None else inc)
        me = (sem_owner.dsem, sem_owner.dcount)
        self.lists[queue].append(("dma", fn, sem_owner.dsem, inc))
        for b in reads:
            b.r.append(me)
        for b in writes:
            b.w = me
            b.r = []
        return me

    def barrier(self):
        deps = [self.eng_dep(e, self.count[e]) for e in self.ENGS if self.count[e] > 0]
        for e in self.ENGS:
            for d in deps:
                if d[2] != e:
                    self._wait(e, d)
            for ds in self.dsems:
                if ds.dcount > 0:
                    self._wait(e, (ds.dsem, ds.dcount))

    def wait_on(self, eng, dep):
        self._wait(eng, dep)

    def replay(self, eng, e):
        for item in self.lists[eng]:
            if item[0] == "wait":
                e.wait_ge(self.semobjs[item[1]], item[2])
            elif item[0] == "op":
                item[1](e).then_inc(self.semobjs[item[2]], 1)
            else:
                ins = item[1](e)
                if item[3] is None:
                    ins.then_inc(self.semobjs[item[2]])
                else:
                    ins.then_inc(self.semobjs[item[2]], item[3])

    def run(self):
        with self.nc.Block() as block:
            @block.tensor
            def _(e):
                self.replay("pe", e)

            @block.scalar
            def _(e):
                self.replay("act", e)

            @block.vector
            def _(e):
                self.replay("dve", e)

            @block.gpsimd
            def _(e):
                self.replay("pool", e)

            @block.sync
            def _(e):
                self.replay("sp", e)


def pieces(lo, hi):
    out = []
    while lo < hi:
        nxt = min(hi, (lo // 512 + 1) * 512)
        out.append((lo, nxt))
        lo = nxt
    return out


def attn_tiles():
    chunks = []
    t = []
    for j in range(8):
        m = [0] if j == 7 else []
        t.append((PA0 + j, 0, 1024, m, "A"))
    for j in range(8):
        t.append((OA0 + j, 128 * j, 1024, [128 * j], "A"))
    chunks.append(t)
    t = []
    for s in range(0, 25):
        m = [0] if s == PB0 + 8 else []
        t.append((s, 0, 640, m, "B"))
    for j in range(5):
        t.append((OB0 + j, 128 * j, 640, [128 * j], "B"))
    chunks.append(t)
    t = []
    for s in range(0, 25):
        t.append((s, 0, 512, [], "B"))
    for j in range(5):
        t.append((OB0 + j, 0, 512, [], "B"))
    for j in range(5, 9):
        t.append((OB0 + j, 128 * (j - 5), 512, [128 * (j - 5)], "B"))
    chunks.append(t)
    return chunks


DEBUG = {"on": False}


def build(stage=4, part="full"):
    nc = bass.Bass("TRN2", target_bir_lowering=False)
    do_L0 = part in ("A", "full")
    do_P3 = part in ("A", "full") and stage >= 3
    do_L1 = part in ("B", "full") and stage >= 3
    declared = []
    dbg_d = None
    if DEBUG["on"]:
        dbg_d = nc.dram_tensor("dbg", [128, 16384], F32, kind="ExternalOutput").ap()
    dbg_state = {"off": 0, "names": []}

    def din(name, shape, dt=F32):
        declared.append(name)
        return nc.dram_tensor(name, list(shape), dt, kind="ExternalInput").ap()

    L0, P3_, L1 = do_L0, do_P3, do_L1
    xT_d = din("xT", [128, NKC, TK]) if L0 else None
    xown_d = din("xown", [128, NOWN, D]) if L0 else None
    maskA_d = din("maskA", [128, 16])
    maskB_d = din("maskB", [128, NS])
    coef_d = din("coef", [16, 4, 8]) if L0 else None
    cos_d = din("cosT", [128, TQ]) if P3_ else None
    sin_d = din("sinT", [128, TQ]) if P3_ else None
    msw_d = din("msw", [128, 2]) if L1 else None
    wqkv0_d = din("wqkv0", [8, 128, 3, NKC, 128]) if L0 else None
    wf0_d = din("wf0", [128, NKC, 16]) if L0 else None
    bf0_d = din("bf0", [16, 1]) if L0 else None
    wo0_d = din("wo0", [128, 8, D]) if L0 else None
    w1h_d = din("w1h", [8, 128, 4, NKC, 128]) if P3_ else None
    wv1_d = din("wv1", [128, NKC, D]) if P3_ else None
    wo1_d = din("wo1", [128, 8, D]) if L1 else None
    gsub_d = din("gsub", [128, 1]) if L1 else None
    lamb_d = din("lamb", [128, 4, 64]) if L1 else None
    use_ffn = [L0 and stage >= 2, L1 and stage >= 4]
    win_d = [din("win%d" % l, [44, 128, NKC, 128]) if use_ffn[l] else None for l in range(2)]
    convw_d = [din("convw%d" % l, [128, 44, 3]) if use_ffn[l] else None for l in range(2)]
    convb_d = [din("convb%d" % l, [128, 44]) if use_ffn[l] else None for l in range(2)]
    wout_d = [din("wout%d" % l, [128, NFC, D]) if use_ffn[l] else None for l in range(2)]
    lng_d = din("lng", [4, 128, D])
    lnb_d = din("lnb", [4, 128, D])
    ident_d = din("ident", [128, 128])
    tri_d = din("tri", [128, 128])
    cmask_d = din("cmask", [128, 128])
    sel_d = din("sel", [16, 16, 65]) if L0 else None
    out_d = nc.dram_tensor("out", [128, NOWN, D], F32, kind="ExternalOutput").ap()
    if part == "full":
        kt_send = nc.dram_tensor("kt_send", [8 * 128, TQ], BF16)
        kt_all = nc.dram_tensor("kt_all", [2 * 8 * 128, TQ], BF16)
        v_send = nc.dram_tensor("v_send", [TQ, D], BF16)
        v_all = nc.dram_tensor("v_all", [2 * TQ, D], BF16)
        qt_dram = nc.dram_tensor("qt_dram", [8 * 128, TQ], BF16)
    elif part == "A":
        kt_send = nc.dram_tensor("kt_send", [8 * 128, TQ], BF16, kind="ExternalOutput")
        v_send = nc.dram_tensor("v_send", [TQ, D], BF16, kind="ExternalOutput")
        qt_dram = nc.dram_tensor("qt_dram", [8 * 128, TQ], BF16, kind="ExternalOutput")
        kt_all = v_all = None
    else:
        declared.extend(["kt_all", "v_all", "qt_dram", "x1own"])
        kt_all = nc.dram_tensor("kt_all", [2 * 8 * 128, TQ], BF16, kind="ExternalInput")
        v_all = nc.dram_tensor("v_all", [2 * TQ, D], BF16, kind="ExternalInput")
        qt_dram = nc.dram_tensor("qt_dram", [8 * 128, TQ], BF16, kind="ExternalInput")
        x1own_d = nc.dram_tensor("x1own", [128, NOWN, D], F32, kind="ExternalInput").ap()
        kt_send = v_send = None

    chunks_struct = attn_tiles()

    with ExitStack() as es:
        T = Tracker(nc, es)

        SB_LO = 16512
        SB_HI = 229344
        X_OFF = SB_LO
        OT_OFF = X_OFF + NOWN * D * 4
        LOC_OFF = OT_OFF + 8 * TQ * 2
        arena = {"ptr": SB_HI, "n": 0}

        def sb_at(name, shape, dt, off):
            arena["n"] += 1
            return nc.alloc_sbuf_tensor_at("%s_%d" % (name, arena["n"]), list(shape), dt, offset=off)

        def sb_size(shape, dt):
            n = 1
            for d_ in shape[1:]:
                n *= d_
            n *= (2 if dt == BF16 else 4)
            return (n + 31) // 32 * 32

        def sb_top(name, shape, dt):
            arena["ptr"] -= sb_size(shape, dt)
            arena["top"] = arena["ptr"]
            return sb_at(name, shape, dt, arena["ptr"])

        class Scope:
            def __init__(self, start=None, parent=None):
                self.ptr = parent.ptr if parent is not None else start
                self.es = ExitStack()

            def enter_context(self, cm):
                return self.es.enter_context(cm)

            def __enter__(self):
                self.es.__enter__()
                return self

            def __exit__(self, *a):
                return self.es.__exit__(*a)

        def sb(scope, name, shape, dt):
            if scope is top:
                return sb_top(name, shape, dt)
            off = scope.ptr
            scope.ptr += sb_size(shape, dt)
            assert scope.ptr <= arena["top"], ("SBUF overflow", name, scope.ptr, arena["top"])
            return sb_at(name, shape, dt, off)

        def psum(scope, name, shape, dt):
            arena["n"] += 1
            return scope.enter_context(nc.psum_tensor("%s_%d" % (name, arena["n"]), list(shape), dt))

        def MM(out, lhsT, rhs, start, stop, rd, wr):
            T.op("pe", lambda e: e.matmul(out, lhsT=lhsT, rhs=rhs, start=start, stop=stop,
                                          skip_group_check=True), reads=rd, writes=wr)

        def TR(out, in_, ident, rd, wr):
            T.op("pe", lambda e: e.transpose(out, in_, ident), reads=rd, writes=wr)

        def ACT(out, in_, func, rd, wr, bias=None, scale=None):
            kw = {}
            if bias is not None:
                kw["bias"] = bias
            if scale is not None:
                kw["scale"] = scale
            T.op("act", lambda e: e.activation(out=out, in_=in_, func=func, **kw), reads=rd, writes=wr)

        def DVE(fn, rd, wr):
            T.op("dve", fn, reads=rd, writes=wr)

        def POOL(fn, rd, wr):
            T.op("pool", fn, reads=rd, writes=wr)

        def COPY(out, in_, rd, wr, eng="dve"):
            if eng == "act":
                T.op(eng, lambda e: e.activation(out=out, in_=in_, func=AF.Copy), reads=rd, writes=wr)
            else:
                T.op(eng, lambda e: e.tensor_copy(out=out, in_=in_), reads=rd, writes=wr)

        def TT(out, in0, in1, op, rd, wr, eng="dve"):
            T.op(eng, lambda e: e.tensor_tensor(out=out, in0=in0, in1=in1, op=op), reads=rd, writes=wr)

        def TS(out, in0, s1, s2, op0, op1, rd, wr, eng="dve"):
            if op1 is None:
                T.op(eng, lambda e: e.tensor_scalar(out=out, in0=in0, scalar1=s1, scalar2=None, op0=op0),
                     reads=rd, writes=wr)
            else:
                T.op(eng, lambda e: e.tensor_scalar(out=out, in0=in0, scalar1=s1, scalar2=s2, op0=op0, op1=op1),
                     reads=rd, writes=wr)

        def STT(out, in0, scalar, in1, op0, op1, rd, wr):
            T.op("dve", lambda e: e.scalar_tensor_tensor(out=out, in0=in0, scalar=scalar, in1=in1,
                                                         op0=op0, op1=op1), reads=rd, writes=wr)

        def LOAD(queue, out, in_, owner, wr, rd=()):
            return T.dma(queue, lambda e: e.dma_start(out=out, in_=in_), owner, reads=rd, writes=wr)

        def DUMP(name, scope, ap, n, rd):
            if dbg_d is None:
                return
            stg = sb(scope, "dbgstg", [128, n], F32)
            bst = Buf("dbgstg")
            COPY(stg[:, :], ap, rd, [bst])
            o = dbg_state["off"]
            dbg_state["off"] += n
            dbg_state["names"].append((name, o, n))
            ds_ = DmaSem(T, "dbg")
            T.dma("sp", lambda e: e.dma_start(out=dbg_d[:, o:o + n], in_=stg[:, :], allow_slow_non_contiguous=True), ds_, reads=[bst])
            build.dbg_names = dbg_state["names"]

        top = es
        IDB = sb(top, "IDB", [128, 128], BF16)
        IDF = sb(top, "IDF", [128, 128], F32)
        TRI = sb(top, "TRI", [128, 128], BF16)
        CMK = sb(top, "CMK", [128, 128], BF16)
        ONESB = sb(top, "ONESB", [128, 128], BF16)
        ONESF = sb(top, "ONESF", [128, 128], F32)
        MASKA = sb(top, "MASKA", [128, 16], F32)
        MASKB = sb(top, "MASKB", [128, NS], F32)
        EPS = sb(top, "EPS", [128, 1], F32)
        bC = Buf("consts")
        dC = DmaSem(T, "consts")
        LOAD("pool", IDB[:, :], ident_d[:, :], dC, [bC])
        LOAD("sp", IDF[:, :], ident_d[:, :], dC, [bC])
        LOAD("pool", TRI[:, :], tri_d[:, :], dC, [bC])
        LOAD("pool", CMK[:, :], cmask_d[:, :], dC, [bC])
        LOAD("sp", MASKA[:, :], maskA_d[:, :], dC, [bC])
        LOAD("sp", MASKB[:, :], maskB_d[:, :], dC, [bC])
        bC2 = Buf("consts2")
        DVE(lambda e: e.memset(ONESB[:, :], 1.0), [], [bC2])
        DVE(lambda e: e.memset(ONESF[:, :], 1.0 / 128.0), [], [bC2])
        DVE(lambda e: e.memset(EPS[:, :], LN_EPS), [], [bC2])
        CONST = [bC, bC2]

        XRES = sb_at("XRES", [128, NOWN, D], F32, X_OFF)
        bXRES = [Buf("xres%d" % i) for i in range(NOWN)]
        dXRES = DmaSem(T, "xres")
        OT = sb_at("OT", [128, 8, TQ], BF16, OT_OFF)
        bOT = [[Buf("ot%d_%d" % (p, c)) for c in range(3)] for p in range(8)]

        def layernorm_block(scope_tmps, i, PSY, bPSY, ln_idx, LNG, LNB, bLN):
            STATS, MV, SD, RSTD, NMR, bST = scope_tmps
            xi = XRES[:, i, :]
            dbgln = DEBUG.get("ln") == (ln_idx, i)
            STT(xi, xi, DN_ALPHA, PSY, ALU.mult, ALU.add, [bXRES[i]] + bPSY, [bXRES[i]])
            if dbgln:
                DUMP("ln_y", DEBUG["scope"], xi, 1024, [bXRES[i]])
            for hh in range(2):
                DVE(lambda e, hh=hh: e.bn_stats(out=STATS[:, hh * 6:(hh + 1) * 6], in_=XRES[:, i, hh * 512:(hh + 1) * 512]),
                    [bXRES[i]], [bST[0]])
            DVE(lambda e: e.bn_aggr(out=MV[:, :], in_=STATS[:, :]), [bST[0]], [bST[1]])
            ACT(SD[:, :], MV[:, 1:2], AF.Sqrt, [bST[1]] + CONST, [bST[2]], bias=EPS[:, 0:1], scale=1.0)
            DVE(lambda e: e.reciprocal(out=RSTD[:, :], in_=SD[:, :]), [bST[2]], [bST[3]])
            TS(NMR[:, :], MV[:, 0:1], RSTD[:, 0:1], -1.0, ALU.mult, ALU.mult, [bST[1], bST[3]], [bST[4]])
            if dbgln:
                DUMP("ln_mv", DEBUG["scope"], MV[:, :], 2, [bST[1]])
                DUMP("ln_sd", DEBUG["scope"], SD[:, :], 1, [bST[2]])
                DUMP("ln_rstd", DEBUG["scope"], RSTD[:, :], 1, [bST[3]])
                DUMP("ln_nmr", DEBUG["scope"], NMR[:, :], 1, [bST[4]])
            ACT(xi, xi, AF.Identity, [bXRES[i], bST[3], bST[4]], [bXRES[i]], bias=NMR[:, 0:1], scale=RSTD[:, 0:1])
            if dbgln:
                DUMP("ln_g", DEBUG["scope"], LNG[:, 0:64], 64, [bLN])
                DUMP("ln_b", DEBUG["scope"], LNB[:, 0:64], 64, [bLN])
            TT(xi, xi, LNG[:, :], ALU.mult, [bXRES[i], bLN], [bXRES[i]], eng="pool")
            TT(xi, xi, LNB[:, :], ALU.add, [bXRES[i], bLN], [bXRES[i]], eng="pool")
            if dbgln:
                DUMP("ln_o", DEBUG["scope"], xi, 1024, [bXRES[i]])

        def ln_tmps(scope):
            STATS = sb(scope, "STATS", [128, 12], F32)
            MV = sb(scope, "MV", [128, 2], F32)
            SD = sb(scope, "SD", [128, 1], F32)
            RSTD = sb(scope, "RSTD", [128, 1], F32)
            NMR = sb(scope, "NMR", [128, 1], F32)
            return (STATS, MV, SD, RSTD, NMR, [Buf("st%d" % k) for k in range(5)])

        def outproj_ln_phase(wo_d, ln_idx, tag):
            with Scope(start=LOC_OFF) as ph:
                WO = sb(ph, "WO" + tag, [128, 8, D], BF16)
                LNG = sb(ph, "LNG" + tag, [128, D], F32)
                LNB = sb(ph, "LNB" + tag, [128, D], F32)
                tm = ln_tmps(ph)
                PSY = psum(ph, "PSY" + tag, [128, 2, 1024], F32)
                bPS = [Buf("psy0"), Buf("psy1")]
                bW = Buf("wo")
                bLN = Buf("ln")
                dW = DmaSem(T, "wo" + tag)
                dL = DmaSem(T, "ln" + tag)
                for hh in range(2):
                    LOAD("pool", WO[:, hh * 4:(hh + 1) * 4, :], wo_d[:, hh * 4:(hh + 1) * 4, :], dW, [bW])
                LOAD("sp", LNG[:, :], lng_d[ln_idx], dL, [bLN])
                LOAD("sp", LNB[:, :], lnb_d[ln_idx], dL, [bLN])
                for i in range(NOWN):
                    pb = i % 2
                    c = 0 if i < 8 else (1 if i < 13 else 2)
                    for hh in range(2):
                        for pc in range(8):
                            MM(PSY[:, pb, hh * 512:(hh + 1) * 512], OT[:, pc, i * 128:(i + 1) * 128],
                               WO[:, pc, hh * 512:(hh + 1) * 512], pc == 0, pc == 7,
                               [bOT[pc][c], bW], [bPS[pb]])
                    layernorm_block(tm, i, PSY[:, pb, :], [bPS[pb]], ln_idx, LNG, LNB, bLN)
                T.barrier()

        def ffn_phase(l, ln_idx):
            with Scope(start=OT_OFF) as ph:
                WOUT = sb(ph, "WOUT", [128, NFC, D], BF16)
                XMT = sb(ph, "XMT", [128, NKC, 770], BF16)
                HIST = sb(ph, "HIST", [128, NKC, 2], BF16)
                ACTT = sb(ph, "ACTT", [128, NFC, 768], BF16)
                WIN = [sb(ph, "WIN%d" % k, [128, 2, NKC, 128], BF16) for k in range(3)]
                CW = sb(ph, "CW", [128, 44, 3], F32)
                CB = sb(ph, "CB", [128, 44], F32)
                LNG = sb(ph, "LNGf", [128, D], F32)
                LNB = sb(ph, "LNBf", [128, D], F32)
                XB = sb(ph, "XB", [128, D], BF16)
                TG = [sb(ph, "TG%d" % k, [128, 386], F32) for k in range(2)]
                TV = [sb(ph, "TV%d" % k, [128, 386], F32) for k in range(2)]
                SG = sb(ph, "SG", [128, 386], F32)
                tm = ln_tmps(ph)
                DEBUG["scope"] = ph
                PU = psum(ph, "PU", [128, 4, 512], F32)
                PSY = psum(ph, "PSYf", [128, 1, 1024], F32)
                PSBT = psum(ph, "PSBT", [128, 1024], BF16)
                bPU = [Buf("pu%d" % k) for k in range(4)]
                bPSY = [Buf("psyf")]
                bPSBT = Buf("psbt")
                bWOUT, bXMT, bHIST, bXB, bCW, bLN, bSG = (Buf("wout"), Buf("xmt"), Buf("hist"), Buf("xb"),
                                                          Buf("cw"), Buf("lnf"), Buf("sg"))
                bACTT = [Buf("actt%d" % j) for j in range(NFC)]
                bWIN = [Buf("win%d" % k) for k in range(3)]
                bTG = [Buf("tg%d" % k) for k in range(2)]
                bTV = [Buf("tv%d" % k) for k in range(2)]
                dWOUT, dCW, dLN = DmaSem(T, "wout"), DmaSem(T, "cw"), DmaSem(T, "lnf")
                dWIN = [DmaSem(T, "win%d" % k) for k in range(3)]
                for q4 in range(2):
                    LOAD("pool", WOUT[:, q4 * 11:(q4 + 1) * 11, :], wout_d[l][:, q4 * 11:(q4 + 1) * 11, :], dWOUT, [bWOUT])
                LOAD("sp", CW[:, :, :], convw_d[l], dCW, [bCW])
                LOAD("sp", CB[:, :], convb_d[l], dCW, [bCW])
                LOAD("sp", LNG[:, :], lng_d[ln_idx], dLN, [bLN])
                LOAD("sp", LNB[:, :], lnb_d[ln_idx], dLN, [bLN])
                DVE(lambda e: e.memset(HIST[:, :, :], 0.0), [], [bHIST])
                wcount = 0
                for gi, (b0, b1) in enumerate(FFN_GROUPS):
                    n = (b1 - b0) * 128
                    COPY(XMT[:, :, 0:2], HIST[:, :, :], [bHIST], [bXMT])
                    for i in range(b0, b1):
                        COPY(XB[:, :], XRES[:, i, :], [bXRES[i]], [bXB], eng="act")
                        for kc in range(NKC):
                            TR(PSBT[:, kc * 128:(kc + 1) * 128], XB[:, kc * 128:(kc + 1) * 128], IDB[:, :],
                               [bXB] + CONST, [bPSBT])
                        off = 2 + (i - b0) * 128
                        COPY(XMT[:, :, off:off + 128], PSBT[:, :].rearrange("p (k t) -> p k t", k=NKC),
                             [bPSBT], [bXMT])
                    COPY(HIST[:, :, :], XMT[:, :, n:n + 2], [bXMT], [bHIST])
                    if DEBUG.get("ffn") and l == 0 and gi == 0:
                        DUMP("xmt_kc0", ph, XMT[:, 0, :], 770, [bXMT])
                        DUMP("xmt_kc7", ph, XMT[:, 7, :], 770, [bXMT])
                    w = n // 2
                    for j in range(NFC):
                        wb = wcount % 3
                        wcount += 1
                        LOAD("pool", WIN[wb][:, 0, :, :], win_d[l][j], dWIN[wb], [bWIN[wb]])
                        LOAD("pool", WIN[wb][:, 1, :, :], win_d[l][NFC + j], dWIN[wb], [bWIN[wb]])
                        for ci in range(2):
                            c0 = ci * w
                            ub = (j * 2 + ci) % 2
                            for gv in range(2):
                                for kc in range(NKC):
                                    MM(PU[:, ub * 2 + gv, 0:w + 2], WIN[wb][:, gv, kc, :], XMT[:, kc, c0:c0 + w + 2],
                                       kc == 0, kc == NKC - 1, [bWIN[wb], bXMT], [bPU[ub * 2 + gv]])
                            for gv, (TB, bTB) in enumerate(((TG, bTG), (TV, bTV))):
                                fj = j + gv * NFC
                                pu = PU[:, ub * 2 + gv, :]
                                tb, btb = TB[ci], bTB[ci]
                                TS(tb[:, 0:w], pu[:, 2:w + 2], CW[:, fj, 2:3], CB[:, fj:fj + 1], ALU.mult, ALU.add,
                                   [bPU[ub * 2 + gv], bCW], [btb])
                                STT(tb[:, 0:w], pu[:, 1:w + 1], CW[:, fj, 1:2], tb[:, 0:w], ALU.mult, ALU.add,
                                    [bPU[ub * 2 + gv], bCW, btb], [btb])
                                STT(tb[:, 0:w], pu[:, 0:w], CW[:, fj, 0:1], tb[:, 0:w], ALU.mult, ALU.add,
                                    [bPU[ub * 2 + gv], bCW, btb], [btb])
                            if DEBUG.get("ffn") and l == 0 and gi == 0 and j == 0 and ci == 0:
                                DUMP("pug", ph, PU[:, ub * 2 + 0, 0:386], 386, [bPU[ub * 2]])
                                DUMP("tg", ph, TG[ci][:, 0:w], w, [bTG[ci]])
                                DUMP("tv", ph, TV[ci][:, 0:w], w, [bTV[ci]])
                            ACT(SG[:, 0:w], TG[ci][:, 0:w], AF.Silu, [bTG[ci]], [bSG])
                            TT(ACTT[:, j, c0:c0 + w], SG[:, 0:w], TV[ci][:, 0:w], ALU.mult,
                               [bSG, bTV[ci]], [bACTT[j]])
                            if DEBUG.get("ffn") and l == 0 and gi == 0 and j == 0 and ci == 0:
                                DUMP("sg", ph, SG[:, 0:w], w, [bSG])
                                DUMP("actt", ph, ACTT[:, j, c0:c0 + w], w, [bACTT[j]])
                    for i in range(b0, b1):
                        t0 = (i - b0) * 128
                        for hh in range(2):
                            for j in range(NFC):
                                MM(PSY[:, 0, hh * 512:(hh + 1) * 512], ACTT[:, j, t0:t0 + 128],
                                   WOUT[:, j, hh * 512:(hh + 1) * 512], j == 0, j == NFC - 1,
                                   [bACTT[j], bWOUT], bPSY)
                        if DEBUG.get("ffn") and l == 0 and i == 0:
                            DUMP("psy", ph, PSY[:, 0, :], 1024, bPSY)
                        layernorm_block(tm, i, PSY[:, 0, :], bPSY, ln_idx, LNG, LNB, bLN)
                T.barrier()

        if do_L0:
            with Scope(start=LOC_OFF) as p0:
                XT = sb_at("XT", [128, NKC, TK], BF16, X_OFF)
                BIASA = sb(p0, "BIASA", [128, 16, 16], F32)
                BIASB = sb(p0, "BIASB", [128, NS, 16], F32)
                DTQ = sb(p0, "DTQ", [16, TQ], BF16)
                SEL = sb(p0, "SEL", [16, 16, 65], BF16)
                bXT = [Buf("xt%d" % k) for k in range(NKC)]
                dXT = [DmaSem(T, "xt%d" % k) for k in range(NKC)]
                for kc in range(NKC):
                    for hh in range(2):
                        LOAD("pool", XT[:, kc, hh * 2176:(hh + 1) * 2176], xT_d[:, kc, hh * 2176:(hh + 1) * 2176],
                             dXT[kc], [bXT[kc]])
                bSEL = Buf("sel")
                dSEL = DmaSem(T, "sel")
                LOAD("pool", SEL[:, :, :], sel_d[:, :, :], dSEL, [bSEL])
                bBIAS, bDTQ = Buf("bias"), Buf("dtq")
                with Scope(parent=p0) as fg:
                    WF = sb(fg, "WF", [128, NKC, 16], BF16)
                    NBF = sb(fg, "NBF", [16, 1], F32)
                    LT = sb(fg, "LT", [16, TK], F32)
                    DT = sb(fg, "DT", [16, TK], F32)
                    ONES16 = sb(fg, "ONES16", [16, 1152], F32)
                    CAND = sb(fg, "CAND", [16, 8], F32)
                    COEF = sb(fg, "COEF", [16, 4, 8], F32)
                    TMPC = sb(fg, "TMPC", [16, 4, 8], F32)
                    OFF = sb(fg, "OFF", [16, 4], F32)
                    PF = psum(fg, "PF", [128, 2, 512], F32)
                    bWF, bNBF, bLT, bDT, bO16, bCAND, bCOEF, bTMPC, bOFF = [Buf(n) for n in
                                                                       "wf nbf lt dt o16 cand coef tmpc off".split()]
                    bPF = [Buf("pf0"), Buf("pf1")]
                    LOAD("pool", WF[:, :, :], wf0_d[:, :, :], DmaSem(T, "fgw"), [bWF])
                    LOAD("sp", NBF[:, :], bf0_d[:, :], DmaSem(T, "fgb"), [bNBF])
                    LOAD("sp", COEF[:, :, :], coef_d[:, :, :], DmaSem(T, "fgc"), [bCOEF])
                    TS(NBF[:, :], NBF[:, :], -1.0, None, ALU.mult, None, [bNBF], [bNBF])
                    DVE(lambda e: e.memset(ONES16[:, :], 1.0), [], [bO16])
                    nch = (TK + 511) // 512
                    for c in range(nch):
                        c0 = c * 512
                        w = min(512, TK - c0)
                        pb = c % 2
                        for kc in range(NKC):
                            MM(PF[0:16, pb, 0:w], WF[:, kc, :], XT[:, kc, c0:c0 + w], kc == 0, kc == NKC - 1,
                               [bWF, bXT[kc]], [bPF[pb]])
                        ACT(LT[:, c0:c0 + w], PF[0:16, pb, 0:w], AF.Exp, [bPF[pb], bNBF], [bLT], bias=NBF[:, 0:1], scale=-1.0)
                    ACT(LT[:, :], LT[:, :], AF.Ln, [bLT], [bLT], bias=1.0, scale=1.0)
                    runs = [(0, 1024), (1024, 1024), (2048, 1152), (3200, 1152)]
                    for R, (r0, rl) in enumerate(runs):
                        DVE(lambda e, r0=r0, rl=rl: e.tensor_tensor_scan(out=DT[:, r0:r0 + rl], data0=ONES16[:, 0:rl],
                                                                          data1=LT[:, r0:r0 + rl], initial=0.0,
                                                                          op0=ALU.mult, op1=ALU.add),
                            [bO16, bLT], [bDT])
                    for R, (r0, rl) in enumerate(runs):
                        COPY(CAND[:, 2 * R:2 * R + 2], DT[:, r0 + 895:r0 + 1024:128], [bDT], [bCAND])
                    for R in range(4):
                        TT(TMPC[:, R, :], COEF[:, R, :], CAND[:, :], ALU.mult, [bCOEF, bCAND], [bTMPC])
                    DVE(lambda e: e.reduce_sum(out=OFF[:, :], in_=TMPC[:, :, :], axis=AX.X), [bTMPC], [bOFF])
                    for R, (r0, rl) in enumerate(runs):
                        TS(DT[:, r0:r0 + rl], DT[:, r0:r0 + rl], OFF[:, R:R + 1], None, ALU.add, None,
                           [bDT, bOFF], [bDT])
                    COPY(DTQ[:, 0:1024], DT[:, OA0 * 128:OA0 * 128 + 1024], [bDT], [bDTQ])
                    COPY(DTQ[:, 1024:TQ], DT[:, OB0 * 128:OB0 * 128 + 1152], [bDT], [bDTQ])
                    for s in range(NS):
                        pb, col = (0, s * 16) if s < 32 else (1, (s - 32) * 16)
                        TR(PF[:, pb, col:col + 16], DT[:, s * 128:(s + 1) * 128], IDF[0:16, 0:16], [bDT] + CONST, [bPF[pb]])
                    for s in range(NS):
                        pb, col = (0, s * 16) if s < 32 else (1, (s - 32) * 16)
                        TS(BIASB[:, s, :], PF[:, pb, col:col + 16], MASKB[:, s:s + 1], None, ALU.add, None,
                           [bPF[pb]] + CONST, [bBIAS])
                        if s < 16:
                            TS(BIASA[:, s, :], PF[:, pb, col:col + 16], MASKA[:, s:s + 1], None, ALU.add, None,
                               [bPF[pb]] + CONST, [bBIAS])
                    T.barrier()

                KT = [sb(p0, "KT%d" % k, [128, TK], BF16) for k in range(3)]
                QT = [sb(p0, "QT%d" % k, [128, TQ], BF16) for k in range(3)]
                VV = [sb(p0, "VV%d" % k, [128, NS, 192], BF16) for k in range(2)]
                WQKV = [sb(p0, "WQKV%d" % k, [128, 3, NKC, 128], BF16) for k in range(2)]
                PT = [sb(p0, "PT%d" % k, [128, 1024], BF16) for k in range(3)]
                OFP = sb(p0, "OFP", [128, 1024], F32)
                RS = sb(p0, "RS", [128, 1024], F32)
                PSS = psum(p0, "PSS", [128, 2, 1024], F32)
                PSO = psum(p0, "PSO", [128, 1024], F32)
                PSP = psum(p0, "PSP", [128, 2, 512], F32)
                bKT = [Buf("kt%d" % k) for k in range(3)]
                bQT = [Buf("qt%d" % k) for k in range(3)]
                bVV = [Buf("vv%d" % k) for k in range(2)]
                bW = [Buf("wqkv%d" % k) for k in range(2)]
                dW = [DmaSem(T, "wqkv%d" % k) for k in range(2)]
                bPT = [Buf("pt%d" % k) for k in range(3)]
                bS = [Buf("s0"), Buf("s1")]
                bO = [Buf("o0"), Buf("o1")]
                bPP = [Buf("pp0"), Buf("pp1")]
                bOFP, bRS = Buf("ofp"), Buf("rs")
                for k in range(3):
                    DVE(lambda e, k=k: e.memset(KT[k][64:65, :], 1.0), [], [bKT[k]])
                for k in range(2):
                    DVE(lambda e, k=k: e.memset(VV[k][:, :, 64:128], 1.0), [], [bVV[k]])

                def load_w(p):
                    LOAD("pool", WQKV[p % 2][:, :, :, :], wqkv0_d[p], dW[p % 2], [bW[p % 2]])

                def v_steps(p):
                    vb = p % 2
                    steps = []
                    for s0 in range(0, NS, 4):
                        def step(s0=s0):
                            ns = min(4, NS - s0)
                            for jj in range(ns):
                                s = s0 + jj
                                for kc in range(NKC):
                                    MM(PSP[:, 0, jj * 128:(jj + 1) * 128], XT[:, kc, s * 128:(s + 1) * 128],
                                       WQKV[vb][:, 2, kc, :], kc == 0, kc == NKC - 1, [bXT[kc], bW[vb]], [bPP[0]])
                            src = PSP[:, 0, 0:ns * 128].rearrange("p (s d) -> p s d", s=ns)
                            COPY(VV[vb][:, s0:s0 + ns, 0:64], src[:, :, 0:64], [bPP[0]], [bVV[vb]])
                            COPY(VV[vb][:, s0:s0 + ns, 128:192], src[:, :, 64:128], [bPP[0]], [bVV[vb]])
                        steps.append(step)
                    return steps

                def kq_steps(p):
                    wb = p % 2
                    re_, ro_ = (2 * p) % 3, (2 * p + 1) % 3
                    steps = []
                    for c0 in range(0, TK, 512):
                        def step(c0=c0):
                            w = min(512, TK - c0)
                            for kc in range(NKC):
                                MM(PSP[:, 0, 0:w], WQKV[wb][:, 1, kc, :], XT[:, kc, c0:c0 + w], kc == 0, kc == NKC - 1,
                                   [bXT[kc], bW[wb]], [bPP[0]])
                            COPY(KT[re_][0:64, c0:c0 + w], PSP[0:64, 0, 0:w], [bPP[0]], [bKT[re_]])
                            COPY(KT[ro_][0:64, c0:c0 + w], PSP[64:128, 0, 0:w], [bPP[0]], [bKT[ro_]])
                        steps.append(step)
                    qchunks = [(q0, 256) for q0 in range(0, 2048, 256)] + [(2048, 128)]
                    for (q0, w) in qchunks:
                        def step(q0=q0, w=w):
                            x0 = (OA0 * 128 + q0) if q0 < 1024 else (OB0 * 128 + q0 - 1024)
                            for kc in range(NKC):
                                MM(PSP[:, 0, 0:w], WQKV[wb][:, 0, kc, :], XT[:, kc, x0:x0 + w], kc == 0, kc == NKC - 1,
                                   [bXT[kc], bW[wb]], [bPP[0]])
                            for e_, r_ in ((0, re_), (1, ro_)):
                                MM(PSP[0:65, 1, e_ * 256:e_ * 256 + w], SEL[:, 2 * p + e_, :], DTQ[:, q0:q0 + w], True, True,
                                   [bSEL, bDTQ], [bPP[1]])
                            COPY(QT[re_][0:64, q0:q0 + w], PSP[0:64, 0, 0:w], [bPP[0]], [bQT[re_]])
                            COPY(QT[ro_][0:64, q0:q0 + w], PSP[64:128, 0, 0:w], [bPP[0]], [bQT[ro_]])
                            COPY(QT[re_][64:65, q0:q0 + w], PSP[64:65, 1, 0:w], [bPP[1]], [bQT[re_]])
                            COPY(QT[ro_][64:65, q0:q0 + w], PSP[64:65, 1, 256:256 + w], [bPP[1]], [bQT[ro_]])
                        steps.append(step)
                    return steps

                state = {"tile": 0}

                def fox_head(h, bg_steps):
                    p, e_ = h // 2, h % 2
                    r = h % 3
                    vb = p % 2
                    ntiles = sum(len(t) for t in chunks_struct)
                    nbg = len(bg_steps)
                    done_bg = 0
                    seen = 0
                    for ci, tiles in enumerate(chunks_struct):
                        qc0, cw = QCHUNKS[ci]
                        pend = None
                        firstbank = {}

                        def emit_pv(t, pbuf):
                            slot, lo, hi, masks, grp = t
                            for (a, b) in pieces(lo, hi):
                                bank = a // 512
                                first = bank not in firstbank
                                firstbank[bank] = True
                                MM(PSO[:, a:b], VV[vb][:, slot, e_ * 64:e_ * 64 + 128], PT[pbuf][:, a:b], first, False,
                                   [bVV[vb], bPT[pbuf]], [bO[bank]])

                        for ti, t in enumerate(tiles):
                            slot, lo, hi, masks, grp = t
                            g = state["tile"]
                            state["tile"] += 1
                            sbuf_i, pbuf = g % 2, g % 3
                            for (a, b) in pieces(lo, hi):
                                bank = a // 512
                                hasmask = any(a <= m < b for m in masks)
                                MM(PSS[:, sbuf_i, a:b], KT[r][0:65, slot * 128:(slot + 1) * 128],
                                   QT[r][0:65, qc0 + a:qc0 + b], True, not hasmask, [bKT[r], bQT[r]], [bS[sbuf_i]])
                                for m in masks:
                                    if a <= m < b:
                                        MM(PSS[:, sbuf_i, m:m + 128], IDB[:, :], TRI[:, :], False, True, CONST, [bS[sbuf_i]])
                            if pend is not None:
                                emit_pv(*pend)
                            bias = (BIASA[:, slot, h:h + 1] if grp == "A" else BIASB[:, slot, h:h + 1])
                            ACT(PT[pbuf][:, lo:hi], PSS[:, sbuf_i, lo:hi], AF.Exp, [bS[sbuf_i], bBIAS], [bPT[pbuf]],
                                bias=bias, scale=0.125)
                            pend = (t, pbuf)
                            seen += 1
                            want = (nbg * seen) // ntiles
                            while done_bg < want:
                                bg_steps[done_bg]()
                                done_bg += 1
                        emit_pv(*pend)
                        nb = (cw + 511) // 512
                        vr = (0, 64) if e_ == 0 else (64, 128)
                        sr = (64, 128) if e_ == 0 else (0, 64)
                        obufs = [bO[k] for k in range(nb)]
                        COPY(OFP[vr[0]:vr[1], 0:cw], PSO[vr[0]:vr[1], 0:cw], obufs, [bOFP])
                        COPY(RS[vr[0]:vr[1], 0:cw], PSO[sr[0]:sr[1], 0:cw], obufs, [bRS])
                        DVE(lambda e, cw=cw, vr=vr: e.reciprocal(out=RS[vr[0]:vr[1], 0:cw], in_=RS[vr[0]:vr[1], 0:cw]),
                            [bRS], [bRS])
                        TT(OT[vr[0]:vr[1], p, qc0:qc0 + cw], OFP[vr[0]:vr[1], 0:cw], RS[vr[0]:vr[1], 0:cw], ALU.mult,
                           [bOFP, bRS], [bOT[p][ci]])
                    while done_bg < nbg:
                        bg_steps[done_bg]()
                        done_bg += 1

                load_w(0)
                for st in v_steps(0) + kq_steps(0):
                    st()
                for p in range(8):
                    if p + 1 < 8:
                        load_w(p + 1)
                    fox_head(2 * p, v_steps(p + 1) if p + 1 < 8 else [])
                    fox_head(2 * p + 1, kq_steps(p + 1) if p + 1 < 8 else [])
                T.barrier()

        for hh in range(2):
            lo, hi = (0, 9) if hh == 0 else (9, NOWN)
            src_x = xown_d if do_L0 else x1own_d
            LOAD("sp", XRES[:, lo:hi, :], src_x[:, lo:hi, :], DmaSem(T, "xres%d" % hh), bXRES[lo:hi])
        if do_L0:
            outproj_ln_phase(wo0_d, 0, "0")
            if stage >= 2:
                ffn_phase(0, 1)

        if stage >= 3:
            bKTS, bVS, bQTD, bKTA, bVA = Buf("kts"), Buf("vs"), Buf("qtd"), Buf("kta"), Buf("va")
            dKTS, dVS, dQTD = DmaSem(T, "kts"), DmaSem(T, "vs"), DmaSem(T, "qtd")
            if do_P3:
                with Scope(start=OT_OFF) as p3:
                    X1T = sb(p3, "X1T", [128, NKC, TQ], BF16)
                    COS = sb(p3, "COS", [128, TQ], F32)
                    SIN = sb(p3, "SIN", [128, TQ], F32)
                    WV1 = sb(p3, "WV1", [128, NKC, D], BF16)
                    W1H = [sb(p3, "W1H%d" % k, [128, 2, NKC, 128], BF16) for k in range(2)]
                    KST = [sb(p3, "KST%d" % k, [128, TQ], BF16) for k in range(2)]
                    VST = [sb(p3, "VST%d" % k, [128, D], BF16) for k in range(2)]
                    XB = sb(p3, "XB3", [128, D], BF16)
                    TA = sb(p3, "TA", [128, 512], F32)
                    TB_ = sb(p3, "TB", [128, 512], F32)
                    PQ = psum(p3, "PQ", [128, 4, 512], F32)
                    PV_ = psum(p3, "PV3", [128, 1, 1024], F32)
                    PSBT = psum(p3, "PSBT3", [128, 1024], BF16)
                    bX1T, bCS, bWV1, bXB, bTA, bTB, bPSBT = [Buf(n) for n in "x1t cs wv1 xb3 ta tb psbt3".split()]
                    bW1H = [Buf("w1h0"), Buf("w1h1")]
                    bKST = [Buf("kst0"), Buf("kst1")]
                    bVST = [Buf("vst0"), Buf("vst1")]
                    bPQ = [Buf("pq%d" % k) for k in range(4)]
                    bPV = [Buf("pv3")]
                    dCS, dWV1 = DmaSem(T, "cs"), DmaSem(T, "wv1")
                    dW1H = [DmaSem(T, "w1h0"), DmaSem(T, "w1h1")]
                    LOAD("sp", COS[:, :], cos_d[:, :], dCS, [bCS])
                    LOAD("sp", SIN[:, :], sin_d[:, :], dCS, [bCS])
                    for hh in range(2):
                        LOAD("pool", WV1[:, hh * 4:(hh + 1) * 4, :], wv1_d[:, hh * 4:(hh + 1) * 4, :], dWV1, [bWV1])
                    for i in range(NOWN):
                        COPY(XB[:, :], XRES[:, i, :], [bXRES[i]], [bXB], eng="act")
                        for kc in range(NKC):
                            TR(PSBT[:, kc * 128:(kc + 1) * 128], XB[:, kc * 128:(kc + 1) * 128], IDB[:, :],
                               [bXB] + CONST, [bPSBT])
                        COPY(X1T[:, :, i * 128:(i + 1) * 128], PSBT[:, :].rearrange("p (k t) -> p k t", k=NKC),
                             [bPSBT], [bX1T])
                    for i in range(NOWN):
                        vb = i % 2
                        for hh in range(2):
                            for kc in range(NKC):
                                MM(PV_[:, 0, hh * 512:(hh + 1) * 512], X1T[:, kc, i * 128:(i + 1) * 128],
                                   WV1[:, kc, hh * 512:(hh + 1) * 512], kc == 0, kc == NKC - 1, [bX1T, bWV1], bPV)
                        COPY(VST[vb][:, :], PV_[:, 0, :], bPV, [bVST[vb]], eng="act")
                        T.dma("sp", lambda e, i=i, vb=vb: e.dma_start(out=v_send[i * 128:(i + 1) * 128, :], in_=VST[vb][:, :]),
                              dVS, reads=[bVST[vb]], writes=[bVS])
                    tchunks = [(c0, min(512, TQ - c0)) for c0 in range(0, TQ, 512)]
                    cnt = 0
                    pcnt = 0
                    for h in range(8):
                        for which in range(2):
                            kb = cnt % 2
                            wb = cnt % 2
                            cnt += 1
                            LOAD("pool", W1H[wb][:, :, :, :], w1h_d[h, :, 2 * which:2 * which + 2, :, :], dW1H[wb], [bW1H[wb]])
                            for (c0, w) in tchunks:
                                pcnt += 1
                                pa, pb_ = 2 * (pcnt % 2), 2 * (pcnt % 2) + 1
                                for v_ in range(2):
                                    for kc in range(NKC):
                                        MM(PQ[:, pa + v_, 0:w], W1H[wb][:, v_, kc, :],
                                           X1T[:, kc, c0:c0 + w], kc == 0, kc == NKC - 1, [bW1H[wb], bX1T],
                                           [bPQ[pa + v_]])
                                TT(TA[:, 0:w], PQ[:, pa, 0:w], COS[:, c0:c0 + w], ALU.mult, [bPQ[pa], bCS], [bTA])
                                TT(TB_[:, 0:w], PQ[:, pb_, 0:w], SIN[:, c0:c0 + w], ALU.mult, [bPQ[pb_], bCS], [bTB])
                                TT(KST[kb][:, c0:c0 + w], TA[:, 0:w], TB_[:, 0:w], ALU.add, [bTA, bTB], [bKST[kb]], eng="pool")
                            if which == 0:
                                T.dma("sp", lambda e, h=h, kb=kb: e.dma_start(out=kt_send[h * 128:(h + 1) * 128, :], in_=KST[kb][:, :]),
                                      dKTS, reads=[bKST[kb]], writes=[bKTS])
                            else:
                                T.dma("sp", lambda e, h=h, kb=kb: e.dma_start(out=qt_dram[h * 128:(h + 1) * 128, :], in_=KST[kb][:, :]),
                                      dQTD, reads=[bKST[kb]], writes=[bQTD])
                    T.barrier()
            if part == "full":
                dCC1, dCC2 = DmaSem(T, "cc1"), DmaSem(T, "cc2")
                T.dma("pool", lambda e: e.collective_compute("AllGather", ALU.bypass, replica_groups=RG,
                                                             ins=[kt_send.ap().opt()], outs=[kt_all.ap().opt()]),
                      dCC1, reads=[bKTS], writes=[bKTA], inc=None)
                T.dma("pool", lambda e: e.collective_compute("AllGather", ALU.bypass, replica_groups=RG,
                                                             ins=[v_send.ap().opt()], outs=[v_all.ap().opt()]),
                      dCC2, reads=[bVS], writes=[bVA], inc=None)

            if do_L1:
                with Scope(start=LOC_OFF) as p4:
                    KX = [sb(p4, "KX%d" % k, [128, TK], BF16) for k in range(2)]
                    VX = [sb(p4, "VX%d" % k, [128, NS, 128], BF16) for k in range(2)]
                    QX = [sb(p4, "QX%d" % k, [128, TQ], BF16) for k in range(2)]
                    PT = [sb(p4, "PT4_%d" % k, [128, 1024], BF16) for k in range(3)]
                    MSW = sb(p4, "MSW", [128, 2], F32)
                    SWT = sb(p4, "SWT", [128, 1152], BF16)
                    SWU = sb(p4, "SWU", [128, 1152], BF16)
                    RS = sb(p4, "RS4", [128, 1024], F32)
                    O1 = sb(p4, "O1", [128, 1024], F32)
                    O2 = sb(p4, "O2", [128, 1024], F32)
                    SQ = sb(p4, "SQ", [128, 1024], F32)
                    LAMB = sb(p4, "LAMB", [128, 4, 64], F32)
                    LPR = sb(p4, "LPR", [128, 2, 64], F32)
                    LS = sb(p4, "LS", [128, 2], F32)
                    NLAM = sb(p4, "NLAM", [128, 1], F32)
                    GS = sb(p4, "GS", [128, 1], F32)
                    EPSR = sb(p4, "EPSR", [128, 1], F32)
                    PSS = psum(p4, "PSS4", [128, 2, 1024], F32)
                    PSOV = psum(p4, "PSOV", [128, 1024], F32)
                    PSOS = psum(p4, "PSOS", [128, 1024], F32)
                    bKX = [Buf("kx0"), Buf("kx1")]
                    bVX = [Buf("vx0"), Buf("vx1")]
                    bQX = [Buf("qx0"), Buf("qx1")]
                    dKX = [DmaSem(T, "kx0"), DmaSem(T, "kx1")]
                    dVX = [DmaSem(T, "vx0"), DmaSem(T, "vx1")]
                    dQX = [DmaSem(T, "qx0"), DmaSem(T, "qx1")]
                    bPT = [Buf("pt4_%d" % k) for k in range(3)]
                    bS = [Buf("s4_0"), Buf("s4_1")]
                    bOV = [Buf("ov0"), Buf("ov1")]
                    bOS = [Buf("os0"), Buf("os1")]
                    bMSW, bSWT, bSWU, bRS, bO1, bO2, bSQ, bLAM, bGS = [Buf(n) for n in
                                                                       "msw swt swu rs4 o1 o2 sq lam gs".split()]
                    LOAD("sp", MSW[:, :], msw_d[:, :], DmaSem(T, "m4a"), [bMSW])
                    LOAD("sp", LAMB[:, :, :], lamb_d[:, :, :], DmaSem(T, "m4b"), [bLAM])
                    LOAD("sp", GS[:, :], gsub_d[:, :], DmaSem(T, "m4c"), [bGS])
                    TS(GS[:, :], GS[:, :], 1.0 - LAMBDA_INIT, None, ALU.mult, None, [bGS], [bGS])
                    DVE(lambda e: e.memset(EPSR[:, :], RMS_EPS), [], [bGS])
                    TT(LPR[:, 0, :], LAMB[:, 0, :], LAMB[:, 1, :], ALU.mult, [bLAM], [bLAM])
                    TT(LPR[:, 1, :], LAMB[:, 2, :], LAMB[:, 3, :], ALU.mult, [bLAM], [bLAM])
                    DVE(lambda e: e.reduce_sum(out=LS[:, :], in_=LPR[:, :, :], axis=AX.X), [bLAM], [bLAM])
                    ACT(LS[:, :], LS[:, :], AF.Exp, [bLAM], [bLAM])
                    TT(NLAM[:, :], LS[:, 1:2], LS[:, 0:1], ALU.subtract, [bLAM], [bLAM])
                    TS(NLAM[:, :], NLAM[:, :], -LAMBDA_INIT, None, ALU.add, None, [bLAM], [bLAM])

                    regions = [(0, 1024), (1024, 1024), (2048, 1152), (3200, 1152)]

                    def load_head(h):
                        kb = h % 2
                        ka = kt_all.ap()
                        va = v_all.ap()
                        for rk in range(2):
                            base = rk * 1024 + h * 128
                            T.dma("sp", lambda e, base=base, rk=rk, kb=kb: e.dma_start(
                                out=KX[kb][:, regions[rk][0]:regions[rk][0] + 1024], in_=ka[base:base + 128, 0:1024]),
                                dKX[kb], reads=[bKTA], writes=[bKX[kb]])
                            T.dma("sp", lambda e, base=base, rk=rk, kb=kb: e.dma_start(
                                out=KX[kb][:, regions[2 + rk][0]:regions[2 + rk][0] + 1152], in_=ka[base:base + 128, 1024:TQ]),
                                dKX[kb], reads=[bKTA], writes=[bKX[kb]])
                            T.dma("sp", lambda e, rk=rk, kb=kb, h=h: e.dma_start(
                                out=VX[kb][:, rk * 8:rk * 8 + 8, :],
                                in_=va[rk * TQ:rk * TQ + 1024, h * 128:(h + 1) * 128].rearrange("(s p) d -> p s d", p=128)),
                                dVX[kb], reads=[bVA], writes=[bVX[kb]])
                            T.dma("sp", lambda e, rk=rk, kb=kb, h=h: e.dma_start(
                                out=VX[kb][:, 16 + rk * 9:16 + rk * 9 + 9, :],
                                in_=va[rk * TQ + 1024:rk * TQ + TQ, h * 128:(h + 1) * 128].rearrange("(s p) d -> p s d", p=128)),
                                dVX[kb], reads=[bVA], writes=[bVX[kb]])
                        T.dma("sp", lambda e, kb=kb, h=h: e.dma_start(out=QX[kb][:, :], in_=qt_dram[h * 128:(h + 1) * 128, :]),
                              dQX[kb], reads=[bQTD], writes=[bQX[kb]])
                        m1, m0 = MSW[:, 0:1], MSW[:, 1:2]
                        for (xr, yr) in ((0, 1), (2, 3)):
                            x0, n_ = regions[xr]
                            y0, _ = regions[yr]
                            X, Y = KX[kb][:, x0:x0 + n_], KX[kb][:, y0:y0 + n_]
                            TS(SWT[:, 0:n_], Y, m1, None, ALU.mult, None, [bKX[kb], bMSW], [bSWT])
                            TS(SWU[:, 0:n_], Y, m0, None, ALU.mult, None, [bKX[kb], bMSW], [bSWU])
                            STT(Y, X, m1, SWU[:, 0:n_], ALU.mult, ALU.add, [bKX[kb], bMSW, bSWU], [bKX[kb]])
                            STT(X, X, m0, SWT[:, 0:n_], ALU.mult, ALU.add, [bKX[kb], bMSW, bSWT], [bKX[kb]])
                        for (xs, ys, ns_) in ((0, 8, 8), (16, 25, 9)):
                            X = VX[kb][:, xs:xs + ns_, :]
                            Y = VX[kb][:, ys:ys + ns_, :]
                            TW = SWT[:, 0:ns_ * 128].rearrange("p (s d) -> p s d", s=ns_)
                            UW = SWU[:, 0:ns_ * 128].rearrange("p (s d) -> p s d", s=ns_)
                            TS(TW, Y, m1, None, ALU.mult, None, [bVX[kb], bMSW], [bSWT])
                            TS(UW, Y, m0, None, ALU.mult, None, [bVX[kb], bMSW], [bSWU])
                            STT(Y, X, m1, UW, ALU.mult, ALU.add, [bVX[kb], bMSW, bSWU], [bVX[kb]])
                            STT(X, X, m0, TW, ALU.mult, ALU.add, [bVX[kb], bMSW, bSWT], [bVX[kb]])

                    state = {"tile": 0}

                    def diff_head(h):
                        kb = h % 2
                        for ci, tiles in enumerate(chunks_struct):
                            qc0, cw = QCHUNKS[ci]
                            nb = (cw + 511) // 512
                            for s_ in range(2):
                                r0 = 64 * s_
                                pend = None
                                firstbank = {}

                                def emit_pv(t, pbuf):
                                    slot, lo, hi, masks, grp = t
                                    for (a, b) in pieces(lo, hi):
                                        bank = a // 512
                                        first = bank not in firstbank
                                        firstbank[bank] = True
                                        MM(PSOV[:, a:b], VX[kb][:, slot, :], PT[pbuf][:, a:b], first, False,
                                           [bVX[kb], bPT[pbuf]], [bOV[bank]])
                                        MM(PSOS[:, a:b], ONESB[:, :], PT[pbuf][:, a:b], first, False,
                                           CONST + [bPT[pbuf]], [bOS[bank]])

                                for t in tiles:
                                    slot, lo, hi, masks, grp = t
                                    g = state["tile"]
                                    state["tile"] += 1
                                    sbuf_i, pbuf = g % 2, g % 3
                                    for (a, b) in pieces(lo, hi):
                                        hasmask = any(a <= m < b for m in masks)
                                        MM(PSS[:, sbuf_i, a:b], KX[kb][r0:r0 + 64, slot * 128:(slot + 1) * 128],
                                           QX[kb][r0:r0 + 64, qc0 + a:qc0 + b], True, not hasmask,
                                           [bKX[kb], bQX[kb]], [bS[sbuf_i]])
                                        for m in masks:
                                            if a <= m < b:
                                                MM(PSS[:, sbuf_i, m:m + 128], IDB[:, :], CMK[:, :], False, True, CONST,
                                                   [bS[sbuf_i]])
                                    if pend is not None:
                                        emit_pv(*pend)
                                    bias = (MASKA[:, slot:slot + 1] if grp == "A" else MASKB[:, slot:slot + 1])
                                    ACT(PT[pbuf][:, lo:hi], PSS[:, sbuf_i, lo:hi], AF.Exp, [bS[sbuf_i]] + CONST, [bPT[pbuf]],
                                        bias=bias, scale=0.125)
                                    pend = (t, pbuf)
                                emit_pv(*pend)
                                OD, bOD = (O1, bO1) if s_ == 0 else (O2, bO2)
                                DVE(lambda e, cw=cw: e.reciprocal(out=RS[:, 0:cw], in_=PSOS[:, 0:cw]),
                                    [bOS[k] for k in range(nb)], [bRS])
                                TT(OD[:, 0:cw], PSOV[:, 0:cw], RS[:, 0:cw], ALU.mult, [bOV[k] for k in range(nb)] + [bRS], [bOD])
                            STT(O1[:, 0:cw], O2[:, 0:cw], NLAM[:, 0:1], O1[:, 0:cw], ALU.mult, ALU.add,
                                [bO1, bO2, bLAM], [bO1])
                            TT(SQ[:, 0:cw], O1[:, 0:cw], O1[:, 0:cw], ALU.mult, [bO1], [bSQ], eng="pool")
                            for (a, b) in pieces(0, cw):
                                MM(PSOV[:, a:b], ONESF[:, :], SQ[:, a:b], True, True, CONST + [bSQ], [bOV[a // 512]])
                            ACT(SQ[:, 0:cw], PSOV[:, 0:cw], AF.Ln, [bOV[k] for k in range(nb)] + [bGS], [bSQ],
                                bias=EPSR[:, 0:1], scale=1.0)
                            ACT(SQ[:, 0:cw], SQ[:, 0:cw], AF.Exp, [bSQ], [bSQ], scale=-0.5)
                            STT(OT[:, h, qc0:qc0 + cw], O1[:, 0:cw], GS[:, 0:1], SQ[:, 0:cw], ALU.mult, ALU.mult,
                                [bO1, bGS, bSQ], [bOT[h][ci]])

                    load_head(0)
                    for h in range(8):
                        if h + 1 < 8:
                            load_head(h + 1)
                        diff_head(h)
                    T.barrier()
                outproj_ln_phase(wo1_d, 2, "1")
                if stage >= 4:
                    ffn_phase(1, 3)

        dOUT = DmaSem(T, "out")
        for hh in range(2):
            lo, hi = (0, 9) if hh == 0 else (9, NOWN)
            T.dma("sp", lambda e, lo=lo, hi=hi: e.dma_start(out=out_d[:, lo:hi, :], in_=XRES[:, lo:hi, :]),
                  dOUT, reads=bXRES[lo:hi])
        T.wait_on("sp", (dOUT.dsem, dOUT.dcount))
        if part == "A" and do_P3:
            for ds_ in (dKTS, dVS, dQTD):
                T.wait_on("sp", (ds_.dsem, ds_.dcount))
        build.declared = list(declared)
        build.stats = dict(count=dict(T.count), nwait=T.nwait, nsem=T.nsem,
                           items={e: len(T.lists[e]) for e in T.ENGS})
        T.run()
    return nc


def _rank_blocks(r):
    pa = OWN_A[1 - r]
    pb = OWN_B[1 - r]
    return pa + OWN_A[r] + pb + OWN_B[r]


def _masks(r):
    mA = np.zeros((16,), np.float32)
    mB = np.zeros((NS,), np.float32)
    if r == 0:
        mA[0:8] = NEG
        mB[PA0 + 0] = NEG
        mB[OB0 + 0] = NEG
    else:
        mA[OA0 + 0] = NEG
        mB[OA0 + 0] = NEG
        mB[PB0:PB0 + 9] = NEG
    return (np.tile(mA[None, :], (128, 1)).astype(np.float32),
            np.tile(mB[None, :], (128, 1)).astype(np.float32))


def _coef(r):
    c = np.zeros((4, 8), np.float32)

    def cand(run, j):
        return run * 2 + (j - 6)
    if r == 0:
        c[0, cand(1, 6)] = 1
        c[2, cand(1, 6)] = 1
        c[2, cand(0, 7)] = 1
        c[3, cand(1, 6)] = 1
        c[3, cand(0, 7)] = 1
        c[3, cand(2, 7)] = 1
    else:
        c[1, cand(0, 6)] = 1
        c[3, cand(0, 6)] = 1
        c[3, cand(1, 7)] = 1
        c[2, cand(0, 6)] = 1
        c[2, cand(1, 7)] = 1
        c[2, cand(3, 7)] = 1
    return np.tile(c[None], (16, 1, 1)).astype(np.float32)


def _perm64(w):
    s = w.shape
    w4 = w.reshape(s[:-1] + (s[-1] // 64, 2, 32))
    return np.ascontiguousarray(w4[..., ::-1, :]).reshape(s)


def _kc_layout(w):
    return np.ascontiguousarray(w.reshape(NKC, 128, -1).transpose(1, 0, 2))


def _prep_shared(inp):
    f = lambda a: np.ascontiguousarray(np.asarray(a, dtype=np.float32))
    sh = {}
    w_in = f(inp["a_w_in"])[0]
    q, k, v = w_in[:, 0:1024], w_in[:, 1024:2048], w_in[:, 2048:3072]
    st = np.stack([_kc_layout(q), _kc_layout(k), _kc_layout(v)], axis=1)
    sh["wqkv0"] = np.ascontiguousarray(st.reshape(128, 3, NKC, 8, 128).transpose(3, 0, 1, 2, 4))
    sh["wf0"] = _kc_layout(w_in[:, 3072:3088])
    sh["bf0"] = f(inp["a_b_f"])[0].reshape(16, 1)
    sh["wo0"] = _kc_layout(f(inp["a_w_out"])[0])
    kvw = f(inp["kv_w"])
    wq = f(inp["b_w_q"])[0]

    def per_head(w):
        w3 = w.reshape(1024, 2, 8, 64).transpose(0, 2, 1, 3).reshape(1024, 8, 128)
        return w3
    kh = per_head(kvw[:, 0:1024])
    qh = per_head(wq)
    st = np.stack([_kc_layout(kh.reshape(1024, 1024)), _kc_layout(_perm64(kh).reshape(1024, 1024)),
                   _kc_layout(qh.reshape(1024, 1024)), _kc_layout(_perm64(qh).reshape(1024, 1024))], axis=1)
    sh["w1h"] = np.ascontiguousarray(st.reshape(128, 4, NKC, 8, 128).transpose(3, 0, 1, 2, 4))
    sh["wv1"] = _kc_layout(kvw[:, 1024:2048])
    sh["wo1"] = _kc_layout(f(inp["b_w_out"])[0])
    sh["gsub"] = f(inp["b_subln_g"])[0].reshape(128, 1)
    sh["lamb"] = np.ascontiguousarray(np.tile(f(inp["b_lambda"])[0][None], (128, 1, 1)))
    win = f(inp["ffn_w_in"])
    winr = np.ascontiguousarray(win.reshape(2, NKC, 128, 44, 128).transpose(0, 3, 2, 1, 4))
    sh["win0"], sh["win1"] = np.ascontiguousarray(winr[0]), np.ascontiguousarray(winr[1])
    cw = f(inp["ffn_conv_w"])
    cwr = np.ascontiguousarray(cw.reshape(2, 3, 44, 128).transpose(0, 3, 2, 1))
    sh["convw0"], sh["convw1"] = np.ascontiguousarray(cwr[0]), np.ascontiguousarray(cwr[1])
    cb = f(inp["ffn_conv_b"])
    cbr = np.ascontiguousarray(cb.reshape(2, 44, 128).transpose(0, 2, 1))
    sh["convb0"], sh["convb1"] = np.ascontiguousarray(cbr[0]), np.ascontiguousarray(cbr[1])
    wout = f(inp["ffn_w_out"])
    wor = np.ascontiguousarray(wout.reshape(2, NFC, 128, 1024).transpose(0, 2, 1, 3))
    sh["wout0"], sh["wout1"] = np.ascontiguousarray(wor[0]), np.ascontiguousarray(wor[1])
    g = np.stack([f(inp["ln_attn_g"])[0], f(inp["ln_ffn_g"])[0], f(inp["ln_attn_g"])[1], f(inp["ln_ffn_g"])[1]])
    b = np.stack([f(inp["ln_attn_b"])[0], f(inp["ln_ffn_b"])[0], f(inp["ln_attn_b"])[1], f(inp["ln_ffn_b"])[1]])
    sh["lng"] = np.ascontiguousarray(np.tile(g[:, None, :], (1, 128, 1)))
    sh["lnb"] = np.ascontiguousarray(np.tile(b[:, None, :], (1, 128, 1)))
    sh["ident"] = np.eye(128, dtype=np.float32)
    kk = np.arange(128)[:, None]
    qq = np.arange(128)[None, :]
    sh["tri"] = np.where(kk <= qq, 0.0, NEG).astype(np.float32)
    sh["cmask"] = np.where((kk // 64) <= (qq // 64), 0.0, NEG).astype(np.float32)
    sel = np.zeros((16, 16, 65), np.float32)
    sel[np.arange(16), np.arange(16), 64] = -8.0
    sh["sel"] = sel
    return sh


def _prep_core(x, b, r):
    blocks = _rank_blocks(r)
    tok = np.concatenate([np.arange(g * 128, (g + 1) * 128) for g in blocks])
    xs = x[b][tok]
    xT = np.ascontiguousarray(xs.T.reshape(NKC, 128, TK).transpose(1, 0, 2))
    own = OWN_A[r] + OWN_B[r]
    otok = np.concatenate([np.arange(g * 128, (g + 1) * 128) for g in own])
    xown = np.ascontiguousarray(x[b][otok].reshape(NOWN, 128, D).transpose(1, 0, 2))
    mA, mB = _masks(r)
    inv_freq = (1.0 / (10000.0 ** (np.arange(0, 64, 2, dtype=np.float32) / np.float32(64)))).astype(np.float32)
    ang = otok.astype(np.float32)[:, None] * inv_freq[None, :]
    fidx = (np.arange(128) % 64) % 32
    sign = np.where((np.arange(128) % 64) < 32, -1.0, 1.0).astype(np.float32)
    cosT = np.ascontiguousarray(np.cos(ang).astype(np.float32).T[fidx])
    sinT = np.ascontiguousarray((np.sin(ang).astype(np.float32).T[fidx]) * sign[:, None])
    msw = np.zeros((128, 2), np.float32)
    msw[:, 0] = 1.0 if r == 0 else 0.0
    msw[:, 1] = 1.0 - msw[:, 0]
    return dict(xT=xT, xown=xown, maskA=mA, maskB=mB, coef=_coef(r), cosT=cosT.astype(np.float32),
                sinT=sinT.astype(np.float32), msw=msw)


_NC_CACHE = {}


def _get_nc(stage, part):
    key = (stage, part)
    if key not in _NC_CACHE:
        nc = build(stage, part)
        _NC_CACHE[key] = (nc, list(build.declared))
    return _NC_CACHE[key]


def _assemble(results):
    out = np.zeros((4, SEQ, D), np.float32)
    for c in range(8):
        b, r = c // 2, c % 2
        o = np.asarray(results[c]["out"])
        own = OWN_A[r] + OWN_B[r]
        halo = own[0] if r == 1 else OWN_B[0][0]
        for i, g in enumerate(own):
            if g == halo:
                continue
            out[b, g * 128:(g + 1) * 128, :] = o[:, i, :]
    return out


def kernel(stage=4, fused=False, **inputs):
    x = np.ascontiguousarray(np.asarray(inputs["x"], dtype=np.float32))
    sh = _prep_shared(inputs)
    full = []
    for c in range(8):
        m = dict(sh)
        m.update(_prep_core(x, c // 2, c % 2))
        full.append(m)
    if fused or stage < 3:
        nc, names = _get_nc(stage, "full")
        res = run_bass_kernel_spmd(nc, [{k: m[k] for k in names} for m in full], core_ids=list(range(8)))
        return _assemble(res.results)
    ncA, namesA = _get_nc(stage, "A")
    resA = run_bass_kernel_spmd(ncA, [{k: m[k] for k in namesA} for m in full], core_ids=list(range(8)))
    ra = resA.results
    ncB, namesB = _get_nc(stage, "B")
    mapsB = []
    for c in range(8):
        m = dict(full[c])
        c0 = (c // 2) * 2
        m["kt_all"] = np.concatenate([np.asarray(ra[c0]["kt_send"]), np.asarray(ra[c0 + 1]["kt_send"])], axis=0)
        m["v_all"] = np.concatenate([np.asarray(ra[c0]["v_send"]), np.asarray(ra[c0 + 1]["v_send"])], axis=0)
        m["qt_dram"] = np.asarray(ra[c]["qt_dram"])
        m["x1own"] = np.asarray(ra[c]["out"])
        mapsB.append({k: m[k] for k in namesB})
    resB = run_bass_kernel_spmd(ncB, mapsB, core_ids=list(range(8)))
    return _assemble(resB.results)
```
